# Optimizing a Trainium2 kernel written in Bass

```python
import jax, jax.numpy as jnp
from jax import lax
import numpy as np

D_MODEL = 4096
BATCH = 2
SEQ = 8192
DEPTH = 2
DEC_BATCH = 8
DEC_SEQ = 16
PAST_LEN = 2048

CHUNK = 64
D_MIX = D_MODEL
D_A = D_MIX // 2
D_B = D_MIX - D_A
HEAD_DIM = 128
H_A = D_A // HEAD_DIM
LRU_BLOCK = 128
H_B = D_B // LRU_BLOCK
CONV_W = 4
LRU_C = 8.0
D_FF = 4 * D_MODEL
N_IN = 4 * D_A + 2 * H_A + 2 * D_B
SPLITS = (3 * D_A, 4 * D_A, 4 * D_A + H_A, 4 * D_A + 2 * H_A, 4 * D_A + 2 * H_A + D_B)
EPS = 1e-6

kernel_name = "hymba_gdn_rglru_stream_step"


def _rmsnorm(x, w):
    xf = x.astype(jnp.float32)
    y = xf * lax.rsqrt(jnp.mean(xf * xf, axis=-1, keepdims=True) + EPS)
    return (y * w.astype(jnp.float32)).astype(x.dtype)


def _l2norm(x):
    xf = x.astype(jnp.float32)
    return xf * lax.rsqrt(jnp.sum(xf * xf, axis=-1, keepdims=True) + EPS)


def _causal_conv(x, prev, w):
    L = x.shape[1]
    xp = jnp.concatenate([prev.astype(x.dtype), x], axis=1)
    y = xp[:, 0:L] * w[0]
    for j in range(1, CONV_W):
        y = y + xp[:, j:j + L] * w[j]
    return y, xp[:, L:]


def _gated_delta_rule(q, k, v, g, beta, s0):
    B, L, H, K = q.shape
    V = v.shape[-1]
    pad = (-L) % CHUNK
    n = (L + pad) // CHUNK

    def blocks(t):
        t = jnp.pad(t, [(0, 0), (0, pad)] + [(0, 0)] * (t.ndim - 2))
        t = jnp.swapaxes(t, 1, 2)
        return t.reshape((B, H, n, CHUNK) + t.shape[3:])

    qc, kc, vc, gc, bc = blocks(q), blocks(k), blocks(v), blocks(g), blocks(beta)
    G = jnp.cumsum(gc, axis=-1)
    idx = jnp.arange(CHUNK)
    causal = idx[:, None] >= idx[None, :]
    strict = idx[:, None] > idx[None, :]
    diff = G[..., :, None] - G[..., None, :]
    decay = jnp.where(causal, jnp.exp(jnp.where(causal, diff, 0.0)), 0.0)
    kb = kc * bc[..., None]
    a_mat = jnp.where(strict, jnp.einsum('bhnik,bhnjk->bhnij', kb, kc) * decay, 0.0)
    rhs = jnp.concatenate([vc * bc[..., None], kb * jnp.exp(G)[..., None]], axis=-1)
    lhs = jnp.eye(CHUNK, dtype=jnp.float32) + a_mat
    sol = lax.linalg.triangular_solve(lhs, rhs, left_side=True, lower=True)
    u, w = sol[..., :V], sol[..., V:]
    qk = jnp.einsum('bhnik,bhnjk->bhnij', qc, kc) * decay
    qg = qc * jnp.exp(G)[..., None]
    kd = kc * jnp.exp(G[..., -1:] - G)[..., None]
    glast = jnp.exp(G[..., -1])

    def step(S, xs):
        u_i, w_i, qk_i, qg_i, kd_i, gl_i = xs
        v_new = u_i - jnp.einsum('bhck,bhkv->bhcv', w_i, S)
        o_i = jnp.einsum('bhck,bhkv->bhcv', qg_i, S) + jnp.einsum('bhij,bhjv->bhiv', qk_i, v_new)
        S = S * gl_i[..., None, None] + jnp.einsum('bhck,bhcv->bhkv', kd_i, v_new)
        return S, o_i

    xs = (jnp.moveaxis(u, 2, 0), jnp.moveaxis(w, 2, 0), jnp.moveaxis(qk, 2, 0),
          jnp.moveaxis(qg, 2, 0), jnp.moveaxis(kd, 2, 0), jnp.moveaxis(glast, 2, 0))
    s_final, o = lax.scan(step, s0, xs)
    o = jnp.moveaxis(o, 0, 2).reshape(B, H, n * CHUNK, V)[:, :, :L]
    return jnp.swapaxes(o, 1, 2), s_final


def _lru_scan(a, b, h0):
    b = b.at[:, 0].add(a[:, 0] * h0)

    def combine(lhs, rhs):
        a_l, b_l = lhs
        a_r, b_r = rhs
        return a_r * a_l, a_r * b_l + b_r

    _, h = lax.associative_scan(combine, (a, b), axis=1)
    return h, h[:, -1]


def _layer(x, conv_d, s_d, conv_l, h_l, p):
    B, L, _ = x.shape
    f32 = jnp.float32
    xn = _rmsnorm(x, p['w_norm_mix'])
    proj = xn @ p['w_in']
    qkv, z, b_logit, a_logit, xl, yl = jnp.split(proj, SPLITS, axis=-1)

    qkv_c, conv_d_new = _causal_conv(qkv, conv_d, p['w_conv_delta'])
    qkv_c = jax.nn.silu(qkv_c).reshape(B, L, 3, H_A, HEAD_DIM)
    q = _l2norm(qkv_c[:, :, 0]) * (HEAD_DIM ** -0.5)
    k = _l2norm(qkv_c[:, :, 1])
    v = qkv_c[:, :, 2].astype(f32)
    beta = jax.nn.sigmoid(b_logit.astype(f32))
    g = -jnp.exp(p['a_log'].astype(f32)) * jax.nn.softplus(a_logit.astype(f32) + p['dt_bias'].astype(f32))
    o, s_d_new = _gated_delta_rule(q, k, v, g, beta, s_d.astype(f32))
    o = _rmsnorm(o, p['w_norm_delta']) * jax.nn.silu(z.astype(f32).reshape(B, L, H_A, HEAD_DIM))
    delta_out = o.reshape(B, L, D_A).astype(x.dtype)

    xc, conv_l_new = _causal_conv(xl, conv_l, p['w_conv_lru'])
    xc = (xc + p['b_conv_lru']).astype(f32).reshape(B, L, H_B, LRU_BLOCK)
    gate_r = jax.nn.sigmoid(jnp.einsum('blhi,hij->blhj', xc, p['w_gate_a'].astype(f32))
                            + p['b_gate_a'].astype(f32).reshape(H_B, LRU_BLOCK))
    gate_i = jax.nn.sigmoid(jnp.einsum('blhi,hij->blhj', xc, p['w_gate_x'].astype(f32))
                            + p['b_gate_x'].astype(f32).reshape(H_B, LRU_BLOCK))
    log_a = -LRU_C * gate_r * jax.nn.softplus(-p['lam'].astype(f32).reshape(H_B, LRU_BLOCK))
    a = jnp.exp(log_a)
    b_in = jnp.sqrt(-jnp.expm1(2.0 * log_a)) * gate_i * xc
    h, h_last = _lru_scan(a.reshape(B, L, D_B), b_in.reshape(B, L, D_B), h_l.astype(f32))
    lru_out = _rmsnorm(jax.nn.gelu(yl.astype(f32)) * h, p['w_norm_lru']).astype(x.dtype)

    x = x + jnp.concatenate([delta_out, lru_out], axis=-1) @ p['w_out']
    hid = jax.nn.relu(_rmsnorm(x, p['w_norm_mlp']) @ p['w_mlp_up'])
    x = x + (hid * hid) @ p['w_mlp_down']
    return x, s_d_new, conv_d_new, h_last, conv_l_new


def _trunk(x, s_d, conv_d, h_l, conv_l, layers, w_norm_final):
    sds, cds, hls, cls = [], [], [], []
    for i in range(DEPTH):
        x, sd, cd, hl, cl = _layer(x, conv_d[i], s_d[i], conv_l[i], h_l[i], layers[i])
        sds.append(sd.astype(x.dtype))
        cds.append(cd.astype(x.dtype))
        hls.append(hl.astype(x.dtype))
        cls.append(cl.astype(x.dtype))
    y = _rmsnorm(x, w_norm_final)
    return y, jnp.stack(sds), jnp.stack(cds), jnp.stack(hls), jnp.stack(cls)


def setup_inputs(seed: int = 0) -> dict:
    key = jax.random.key(seed)
    ks = jax.random.split(key, 26)
    f32 = jnp.float32
    nrm = lambda k, shape, s: jax.random.normal(k, shape, f32) * s
    x_prompt = jax.random.normal(ks[0], (BATCH, SEQ, D_MODEL), f32)
    x_sample = jax.random.normal(ks[1], (DEC_BATCH, DEC_SEQ, D_MODEL), f32)
    state_delta = nrm(ks[2], (DEPTH, DEC_BATCH, H_A, HEAD_DIM, HEAD_DIM), 0.1)
    state_conv_delta = jax.random.normal(ks[3], (DEPTH, DEC_BATCH, CONV_W - 1, 3 * D_A), f32)
    state_lru = nrm(ks[4], (DEPTH, DEC_BATCH, D_B), 0.5)
    state_conv_lru = jax.random.normal(ks[5], (DEPTH, DEC_BATCH, CONV_W - 1, D_B), f32)
    w_norm_mix = 1.0 + nrm(ks[6], (DEPTH, D_MODEL), 0.02)
    w_in = nrm(ks[7], (DEPTH, D_MODEL, N_IN), D_MODEL ** -0.5)
    w_conv_delta = nrm(ks[8], (DEPTH, CONV_W, 3 * D_A), CONV_W ** -0.5)
    a_log = jnp.log(jax.random.uniform(ks[9], (DEPTH, H_A), f32, 1.0, 16.0))
    dt = jnp.exp(jax.random.uniform(ks[10], (DEPTH, H_A), f32, np.log(1e-3), np.log(1e-1)))
    dt_bias = dt + jnp.log(-jnp.expm1(-dt))
    w_norm_delta = 1.0 + nrm(ks[11], (DEPTH, HEAD_DIM), 0.02)
    w_conv_lru = nrm(ks[12], (DEPTH, CONV_W, D_B), CONV_W ** -0.5)
    b_conv_lru = nrm(ks[13], (DEPTH, D_B), 0.02)
    w_gate_a = nrm(ks[14], (DEPTH, H_B, LRU_BLOCK, LRU_BLOCK), LRU_BLOCK ** -0.5)
    b_gate_a = nrm(ks[15], (DEPTH, D_B), 0.02)
    w_gate_x = nrm(ks[16], (DEPTH, H_B, LRU_BLOCK, LRU_BLOCK), LRU_BLOCK ** -0.5)
    b_gate_x = nrm(ks[17], (DEPTH, D_B), 0.02)
    a_max = jax.random.uniform(ks[18], (DEPTH, D_B), f32, 0.9, 0.999)
    s = a_max ** (1.0 / LRU_C)
    lam = jnp.log(s) - jnp.log1p(-s)
    w_norm_lru = 1.0 + nrm(ks[19], (DEPTH, D_B), 0.02)
    w_out = nrm(ks[20], (DEPTH, D_MIX, D_MODEL), D_MIX ** -0.5)
    w_norm_mlp = 1.0 + nrm(ks[21], (DEPTH, D_MODEL), 0.02)
    w_mlp_up = nrm(ks[22], (DEPTH, D_MODEL, D_FF), D_MODEL ** -0.5)
    w_mlp_down = nrm(ks[23], (DEPTH, D_FF, D_MODEL), D_FF ** -0.5)
    w_norm_final = 1.0 + nrm(ks[24], (D_MODEL,), 0.02)
    return {"x_prompt": x_prompt, "x_sample": x_sample,
            "state_delta": state_delta, "state_conv_delta": state_conv_delta,
            "state_lru": state_lru, "state_conv_lru": state_conv_lru,
            "w_norm_mix": w_norm_mix, "w_in": w_in, "w_conv_delta": w_conv_delta,
            "a_log": a_log, "dt_bias": dt_bias, "w_norm_delta": w_norm_delta,
            "w_conv_lru": w_conv_lru, "b_conv_lru": b_conv_lru,
            "w_gate_a": w_gate_a, "b_gate_a": b_gate_a, "w_gate_x": w_gate_x, "b_gate_x": b_gate_x,
            "lam": lam, "w_norm_lru": w_norm_lru, "w_out": w_out,
            "w_norm_mlp": w_norm_mlp, "w_mlp_up": w_mlp_up, "w_mlp_down": w_mlp_down,
            "w_norm_final": w_norm_final}


def reference(x_prompt, x_sample, state_delta, state_conv_delta, state_lru, state_conv_lru,
              w_norm_mix, w_in, w_conv_delta, a_log, dt_bias, w_norm_delta,
              w_conv_lru, b_conv_lru, w_gate_a, b_gate_a, w_gate_x, b_gate_x,
              lam, w_norm_lru, w_out, w_norm_mlp, w_mlp_up, w_mlp_down, w_norm_final):
    layers = [dict(w_norm_mix=w_norm_mix[i], w_in=w_in[i], w_conv_delta=w_conv_delta[i],
                   a_log=a_log[i], dt_bias=dt_bias[i], w_norm_delta=w_norm_delta[i],
                   w_conv_lru=w_conv_lru[i], b_conv_lru=b_conv_lru[i],
                   w_gate_a=w_gate_a[i], b_gate_a=b_gate_a[i],
                   w_gate_x=w_gate_x[i], b_gate_x=b_gate_x[i],
                   lam=lam[i], w_norm_lru=w_norm_lru[i], w_out=w_out[i],
                   w_norm_mlp=w_norm_mlp[i], w_mlp_up=w_mlp_up[i], w_mlp_down=w_mlp_down[i])
              for i in range(DEPTH)]
    bp = x_prompt.shape[0]
    dt_x = x_prompt.dtype
    z_sd = jnp.zeros((DEPTH, bp, H_A, HEAD_DIM, HEAD_DIM), jnp.float32)
    z_cd = jnp.zeros((DEPTH, bp, CONV_W - 1, 3 * D_A), dt_x)
    z_hl = jnp.zeros((DEPTH, bp, D_B), jnp.float32)
    z_cl = jnp.zeros((DEPTH, bp, CONV_W - 1, D_B), dt_x)
    y_prompt, p_delta, p_conv_delta, p_lru, p_conv_lru = _trunk(
        x_prompt, z_sd, z_cd, z_hl, z_cl, layers, w_norm_final)
    y_sample, s_delta, s_conv_delta, s_lru, s_conv_lru = _trunk(
        x_sample, state_delta, state_conv_delta, state_lru, state_conv_lru, layers, w_norm_final)
    return (y_prompt, y_sample, p_delta, p_conv_delta, p_lru, p_conv_lru,
            s_delta, s_conv_delta, s_lru, s_conv_lru)
```

```python
import contextlib
import numpy as np
import concourse.bass as bass
import concourse.mybir as mybir
from concourse.bass_utils import run_bass_kernel_spmd

F32 = mybir.dt.float32
BF16 = mybir.dt.bfloat16
AF = mybir.ActivationFunctionType
ALU = mybir.AluOpType
AX = mybir.AxisListType

NCORE = 8
D = 4096
TOK = 2064
NTOK = NCORE * TOK
DEPTH = 2
TILES = [(0, 512), (512, 512), (1024, 512), (1536, 512), (2048, 16)]
NW = 1664
NS = 298
EPS = 1e-6
ENGS = ("pe", "act", "dve", "pool", "sp")
NDSEM = 12

O_WNMIX, O_WNMLP, O_WNL, O_WCD, O_GC, O_WCL, O_BCL, O_BGA, O_BGX, O_LAM, O_WND, O_WNF = \
    0, 32, 64, 96, 120, 122, 130, 132, 134, 136, 138, 266
C_ID, C_NEGU, C_POSL, C_SU, C_HM, C_MROW, C_ONES, C_SEL, C_TOT = 0, 128, 256, 384, 512, 514, 1026, 1154, 1666


class Prog:
    def __init__(self, nc, stack):
        self.nc = nc
        self.stack = stack
        self.q = {e: [] for e in ENGS}
        self.ms = {}
        self.mc = {}
        self.nsem = 0
        for e in ENGS:
            self._newsem(e)
        self.waited = {e: {} for e in ENGS}
        self.W = {}
        self.R = {}
        self.dsems = {e: [] for e in ENGS}
        self.dnext = {e: 0 for e in ENGS}
        self.ccsem = self._sem("cc")
        self.ccn = 0

    def _sem(self, name):
        self.nsem += 1
        return self.stack.enter_context(self.nc.semaphore(f"{name}_{self.nsem}"))

    def _newsem(self, e):
        self.ms[e] = self._sem("m" + e)
        self.mc[e] = 0

    def _wait(self, e, tok):
        sem, v = tok
        k = id(sem)
        if self.waited[e].get(k, 0) >= v:
            return
        self.waited[e][k] = v
        self.q[e].append(lambda eng, sem=sem, v=v: eng.wait_ge(sem, v))

    def _deps(self, e, r, w, a):
        for n in r:
            for t in self.W.get(n, {}).values():
                self._wait(e, t)
        for n in w:
            for t in self.W.get(n, {}).values():
                self._wait(e, t)
            for t in self.R.get(n, {}).values():
                self._wait(e, t)
        for n in a:
            for t in self.R.get(n, {}).values():
                self._wait(e, t)

    @staticmethod
    def _add(d, tok):
        k = id(tok[0])
        if k not in d or d[k][1] < tok[1]:
            d[k] = tok

    def _record(self, tok, r, w, a):
        for n in r:
            self._add(self.R.setdefault(n, {}), tok)
        for n in w:
            self.W[n] = {id(tok[0]): tok}
            self.R[n] = {}
        for n in a:
            self._add(self.W.setdefault(n, {}), tok)

    def op(self, e, fn, r=(), w=(), a=()):
        self._deps(e, r, w, a)
        if self.mc[e] >= 30000:
            self._newsem(e)
        self.mc[e] += 1
        sem = self.ms[e]
        tok = (sem, self.mc[e])
        self.q[e].append(lambda eng, fn=fn, sem=sem: fn(eng).then_inc(sem, 1))
        self._record(tok, r, w, a)
        return tok

    def dma(self, e, out, in_, r=(), w=(), a=()):
        self._deps(e, r, w, a)
        i = self.dnext[e] % NDSEM
        self.dnext[e] += 1
        pool = self.dsems[e]
        if len(pool) <= i:
            pool.append([self._sem("d" + e), 0])
        s = pool[i]
        if s[1] > 0:
            self._wait(e, (s[0], s[1]))
        s[1] += 16
        sem = s[0]
        tok = (sem, s[1])
        def _do(eng, out=out, in_=in_, sem=sem):
            src = in_(eng) if callable(in_) else in_
            try:
                return eng.dma_start(out=out, in_=src).then_inc(sem, 16)
            except Exception:
                print("DMA FAIL", out.shape, src.shape, out.ap, src.ap, r, w, a)
                raise
        self.q[e].append(_do)
        self._record(tok, r, w, a)
        return tok

    def nop(self, e, r=(), w=(), a=()):
        if e == "sp":
            return self.dma("sp", self.nop_b, self.nop_a, r=r, w=w, a=a)
        return self.op(e, lambda eng: eng.engine_nop(), r=r, w=w, a=a)

    def allgather(self, in_ap, out_ap, rname, wname):
        e = "pool"
        self._deps(e, [rname], [wname], [])
        self.ccn += 1
        sem = self.ccsem
        n = self.ccn
        self.q[e].append(lambda eng, sem=sem: eng.collective_compute(
            "AllGather", ALU.bypass, replica_groups=[list(range(NCORE))],
            ins=[in_ap.opt()], outs=[out_ap.opt()]).then_inc(sem))
        self._wait(e, (sem, n))
        self._record((sem, n), [rname], [], [])
        return self.op(e, lambda eng: eng.engine_nop(), w=[wname])


def build_program(debug=None):
    nc = bass.Bass("TRN2", target_bir_lowering=False)
    dt = nc.dram_tensor
    xT_in = dt("xT", [D, TOK], F32, kind="ExternalInput").ap()
    w_in = dt("w_in", [DEPTH, D, NW], F32, kind="ExternalInput").ap()
    big = debug not in ("A", "AG", "A2", "B", "Bsim")
    w_out = dt("w_out", [DEPTH, D, D], F32, kind="ExternalInput").ap() if big else None
    w_up = dt("w_up", [DEPTH, D, 4 * D], F32, kind="ExternalInput").ap() if big else None
    w_dn = dt("w_dn", [DEPTH, 4 * D, D], F32, kind="ExternalInput").ap() if big else None
    sp_in = dt("sp", [DEPTH, 128, NS], F32, kind="ExternalInput").ap()
    wg_in = dt("wg", [DEPTH, 128, 4, 128], F32, kind="ExternalInput").ap()
    cst_in = dt("cst", [128, C_TOT], F32, kind="ExternalInput").ap()
    sd_in = dt("sd", [DEPTH, 8, 128, 2, 128], F32, kind="ExternalInput").ap()
    scd_in = dt("scd", [DEPTH, 8, 128, 6, 3], F32, kind="ExternalInput").ap()
    sl_in = dt("sl", [DEPTH, 8, 128, 2], F32, kind="ExternalInput").ap()
    scl_in = dt("scl", [DEPTH, 8, 128, 2, 3], F32, kind="ExternalInput").ap()

    yT = dt("yT", [D, TOK], F32, kind="ExternalOutput").ap()
    o_sd = dt("o_sd", [DEPTH, 10, 128, 2, 128], F32, kind="ExternalOutput").ap()
    o_cd = dt("o_cd", [DEPTH, 10, 128, 6, 3], F32, kind="ExternalOutput").ap()
    o_l = dt("o_l", [DEPTH, 10, 128, 2], F32, kind="ExternalOutput").ap()
    o_cl = dt("o_cl", [DEPTH, 10, 128, 2, 3], F32, kind="ExternalOutput").ap()

    xT1 = dt("xT1", [D, TOK], F32).ap()
    xn_loc = dt("xn_loc", [D, TOK], BF16).ap()
    xn_all = dt("xn_all", [NCORE * D, TOK], BF16).ap()
    proj = dt("proj", [NW, NTOK], F32, kind=("ExternalOutput" if debug in ("A2", "B") else ("ExternalInput" if debug == "Bsim" else "Internal"))).ap()
    mix_loc = dt("mix_loc", [512, NTOK], BF16).ap()
    mix_dbg = dt("mix_dbg", [512, NTOK], BF16, kind="ExternalOutput").ap() if debug in ("B", "Bsim") else None
    xn_dbg = dt("xn_dbg", [D, TOK], BF16, kind="ExternalOutput").ap() if debug in ("A", "AG") else None
    xa_dbg = dt("xa_dbg", [NCORE * D, 64], BF16, kind="ExternalOutput").ap() if debug == "AG" else None
    mix_all = dt("mix_all", [NCORE * 512, NTOK], BF16).ap()
    wb_in = dt("wb_in", [DEPTH, D, NW], BF16).ap()
    wb_out = dt("wb_out", [DEPTH, 8, 128, 32, 512], BF16).ap()
    wb_up = dt("wb_up", [DEPTH, 32, 128, 32, 512], BF16).ap()
    wb_dn = dt("wb_dn", [DEPTH, 4, 8, 128, 32, 512], BF16).ap()

    stack = contextlib.ExitStack()
    with stack:
        P = Prog(nc, stack)
        P.nop_a = cst_in[0:1, 0:16]
        P.nop_b = dt("nop_b", [1, 16], F32).ap()[:, :]
        sb = lambda name, shape, dtp: stack.enter_context(nc.sbuf_tensor(name, shape, dtp))
        arena = sb("arena", [128, 44000], F32)
        cst = sb("cst_sb", [128, C_TOT], F32)
        spt = sb("sp_sb", [128, DEPTH, NS], F32)
        ident_b = sb("ident_b", [128, 128], BF16)
        ones_b = sb("ones_b", [128, 128], BF16)
        epsc = sb("epsc", [128, 2], F32)
        pss = [stack.enter_context(nc.psum_tensor(f"ps{i}", [128, 512], F32)) for i in range(8)]
        pb = pss[7]

        ident_f = cst[:, C_ID:C_ID + 128]
        negU = cst[:, C_NEGU:C_NEGU + 128]
        posL = cst[:, C_POSL:C_POSL + 128]
        strictU = cst[:, C_SU:C_SU + 128]
        hmask = cst[:, C_HM:C_HM + 2]
        mrow = cst[:, C_MROW:C_MROW + 512]
        ones_f = cst[:, C_ONES:C_ONES + 128]

        def f32v(off, shape):
            n = int(np.prod(shape[1:]))
            ap = arena[:, off:off + n]
            if len(shape) == 3:
                ap = ap.rearrange("p (a b) -> p a b", a=shape[1])
            elif len(shape) == 4:
                ap = ap.rearrange("p (a b c) -> p a b c", a=shape[1], b=shape[2])
            return ap

        def bf16v(off, shape):
            n = int(np.prod(shape[1:]))
            assert n % 2 == 0
            ap = arena[:, off:off + n // 2].bitcast(BF16)
            if len(shape) == 3:
                ap = ap.rearrange("p (a b) -> p a b", a=shape[1])
            elif len(shape) == 4:
                ap = ap.rearrange("p (a b c) -> p a b c", a=shape[1], b=shape[2])
            return ap

        P.q["sp"].append(lambda eng: setattr(P, "pidv", eng.partition_id()))
        P.dma("sp", cst[:], cst_in[:, :], w=["cst"])
        P.dma("sp", spt[:], sp_in.rearrange("l p n -> p l n"), w=["spt"])
        P.op("dve", lambda e: e.tensor_copy(out=ident_b[:], in_=ident_f), r=["cst"], w=["ident_b"])
        P.op("dve", lambda e: e.tensor_copy(out=ones_b[:], in_=ones_f), r=["cst"], w=["ones_b"])
        P.op("dve", lambda e: e.memset(epsc[:, 0:1], EPS), w=["epsc"])
        P.op("dve", lambda e: e.memset(epsc[:, 1:2], 1.0), a=["epsc"])

        def spv(l, off, n):
            return spt[:, l, off:off + n]

        xt = f32v(0, [128, 32, 512])
        bA = bf16v(16384, [128, 32, 512])
        bB = bf16v(24576, [128, 32, 512])
        rstd = f32v(32768, [128, 512])
        rtmp = f32v(33280, [128, 512])
        WR = [bf16v(33792 + i * 4096, [128, 16, 512]) for i in range(2)]
        stg = [f32v(33792 + i * 512, [128, 512]) for i in range(4)]
        win = bf16v(0, [128, 32, NW])
        xin = [bf16v(26624, [128, 32, 512])]
        stg = [f32v(34816 + i * 512, [128, 512]) for i in range(4)]


        def barrier():
            P.nop("dve", w=["ARENA"])

        AR = ["ARENA"]

        def cast_win_dram(l):
            for c in range(8):
                P.dma("pool", wb_in[l, c * 512:(c + 1) * 512, :], w_in[l, c * 512:(c + 1) * 512, :], a=[f"wb_in{l}"])

        def cast_win(l):
            for c in range(32):
                P.dma("sp", win[:, c, :], wb_in[l, c * 128:(c + 1) * 128, :], r=[f"wb_in{l}"] + AR, a=["win"])

        def cast_dense(l):
            for g in range(8):
                P.dma("pool", wb_out[l, g], w_out[l][:, g * 512:(g + 1) * 512].rearrange("(c p) j -> p c j", p=128), a=[f"wb_out{l}"])
            for g in range(32):
                P.dma("pool", wb_up[l, g], w_up[l][:, g * 512:(g + 1) * 512].rearrange("(c p) j -> p c j", p=128), a=[f"wb_up{l}"])
            for q in range(4):
                for g in range(8):
                    P.dma("pool", wb_dn[l, q, g],
                          w_dn[l][q * 4096:(q + 1) * 4096, g * 512:(g + 1) * 512].rearrange("(c p) j -> p c j", p=128),
                          a=[f"wb_dn{l}"])

        def phaseA(l, src_ap, src_name):
            srcv = src_ap.rearrange("(c p) t -> p c t", p=128)
            dstv = xn_loc.rearrange("(c p) t -> p c t", p=128)
            wn = spv(l, O_WNMIX, 32)
            for (t0, T) in TILES:
                P.dma("sp", xt[:, :, :T], srcv[:, :, t0:t0 + T], r=[src_name] + AR, w=["xt", "src_t"])
                for h4 in range(4):
                    P.op("act", lambda e, h4=h4, T=T: e.activation(out=bB[:, h4 * 8:(h4 + 1) * 8, :T], in_=xt[:, h4 * 8:(h4 + 1) * 8, :T], func=AF.Square),
                         r=["xt"] + AR, w=["bB"] if h4 == 0 else (), a=() if h4 == 0 else ["bB"])
                for c in range(32):
                    P.op("pe", lambda e, c=c, T=T: e.matmul(pss[4][:, :T], lhsT=ones_b[:], rhs=bB[:, c, :T], start=(c == 0), stop=(c == 31)),
                         r=["bB", "ones_b"] + AR, w=["ps4"] if c == 0 else (), a=() if c == 0 else ["ps4"])
                P.op("act", lambda e, T=T: e.activation(out=rstd[:, :T], in_=pss[4][:, :T], func=AF.Sqrt, bias=epsc[:, 0:1], scale=1.0 / D),
                     r=["ps4", "epsc"] + AR, w=["rstd"])
                P.op("dve", lambda e, T=T: e.reciprocal(out=rstd[:, :T], in_=rstd[:, :T]), r=AR, w=["rstd"])
                for c in range(32):
                    P.op("dve", lambda e, c=c, T=T: e.scalar_tensor_tensor(out=bA[:, c, :T], in0=xt[:, c, :T], scalar=wn[:, c:c + 1], in1=rstd[:, :T],
                                                                          op0=ALU.mult, op1=ALU.mult),
                         r=["xt", "rstd", "spt"] + AR, w=["bA"] if c == 0 else (), a=() if c == 0 else ["bA"])
                P.dma("sp", dstv[:, :, t0:t0 + T], bA[:, :, :T], r=["bA"] + AR, a=["xn_loc"])

        def phaseA2(l):
            xav = xn_all.rearrange("(r c p) t -> r p c t", r=NCORE, p=128)
            k = 0
            for r in range(NCORE):
                for (t0, T) in TILES:
                    xi = xin[0]
                    P.dma("sp", xi[:, :, :T], xav[r][:, :, t0:t0 + T], r=["xn_all"] + AR, w=["xin"])
                    for n in range(13):
                        ps = pss[n % 4]
                        psn = f"ps{n % 4}"
                        for c in range(32):
                            P.op("pe", lambda e, c=c, n=n, T=T, ps=ps, xi=xi: e.matmul(ps[:, :T], lhsT=win[:, c, n * 128:(n + 1) * 128], rhs=xi[:, c, :T],
                                                                                     start=(c == 0), stop=(c == 31)),
                                 r=["win", "xin"] + AR, w=[psn] if c == 0 else (), a=() if c == 0 else [psn])
                        st = stg[n % 4]
                        stn = f"stg{n % 4}"
                        if n % 2 == 0:
                            P.op("act", lambda e, st=st, ps=ps, T=T: e.activation(out=st[:, :T], in_=ps[:, :T], func=AF.Copy), r=[psn] + AR, w=[stn])
                        else:
                            P.op("dve", lambda e, st=st, ps=ps, T=T: e.tensor_copy(out=st[:, :T], in_=ps[:, :T]), r=[psn] + AR, w=[stn])
                        P.dma("sp", proj[n * 128:(n + 1) * 128, r * TOK + t0:r * TOK + t0 + T], st[:, :T], r=[stn] + AR, a=["proj"])
                    k += 1

        wr_cnt = [0]

        def stream_mm(wsrc_blocks, wname, rhs_t, rhs_name, T, evac):
            for g, blk in enumerate(wsrc_blocks):
                slots = []
                for half in range(2):
                    s = wr_cnt[0] % 2
                    wr_cnt[0] += 1
                    P.dma("sp", WR[s][:], blk[:, half * 16:(half + 1) * 16, :], r=[wname] + AR, w=[f"wr{s}"])
                    slots.append(s)
                    for j in range(4):
                        ps = pss[j]
                        psn = f"ps{j}"
                        for cc in range(16):
                            c = half * 16 + cc
                            first = (c == 0)
                            P.op("pe", lambda e, s=s, cc=cc, c=c, j=j, ps=ps, T=T: e.matmul(ps[:, :T], lhsT=WR[s][:, cc, j * 128:(j + 1) * 128], rhs=rhs_t[:, c, :T],
                                                                                        start=(c == 0), stop=(c == 31)),
                                 r=[f"wr{s}", rhs_name] + AR, w=[psn] if first else (), a=() if first else [psn])
                for j in range(4):
                    evac(g * 4 + j, pss[j], f"ps{j}")

        def phaseC(l, cur_ap, cur_name, dst_ap, dst_name, final):
            curv = cur_ap.rearrange("(c p) t -> p c t", p=128)
            dstv = dst_ap.rearrange("(c p) t -> p c t", p=128)
            mav4 = mix_all.rearrange("(cc p) (r t) -> r p cc t", p=128, r=NCORE)
            wnl = spv(l, O_WNL, 32)
            wnm = spv(l, O_WNMLP, 32)
            wnf = spv(l, O_WNF, 32)
            lru_chunks = [s * 4 + 2 + g for s in range(NCORE) for g in range(2)]
            for (t0, T) in TILES:
                P.dma("sp", xt[:, :, :T], curv[:, :, t0:t0 + T], r=[cur_name] + AR, w=["xt"])
                P.dma("sp", bA[:, :, :T], (lambda eng, t0=t0, T=T: mav4[bass.ds(P.pidv, 1)].rearrange("o p c t -> (o p) c t")[:, :, t0:t0 + T]), r=["mix_all"] + AR, w=["bA"])
                for i, c in enumerate(lru_chunks):
                    P.op("act", lambda e, i=i, c=c, T=T: e.activation(out=bB[:, i, :T], in_=bA[:, c, :T], func=AF.Square),
                         r=["bA"] + AR, w=["bB"] if i == 0 else (), a=() if i == 0 else ["bB"])
                for i in range(16):
                    P.op("pe", lambda e, i=i, T=T: e.matmul(pss[4][:, :T], lhsT=ones_b[:], rhs=bB[:, i, :T], start=(i == 0), stop=(i == 15)),
                         r=["bB", "ones_b"] + AR, w=["ps4"] if i == 0 else (), a=() if i == 0 else ["ps4"])
                P.op("act", lambda e, T=T: e.activation(out=rstd[:, :T], in_=pss[4][:, :T], func=AF.Sqrt, bias=epsc[:, 0:1], scale=1.0 / 2048),
                     r=["ps4", "epsc"] + AR, w=["rstd"])
                P.op("dve", lambda e, T=T: e.reciprocal(out=rstd[:, :T], in_=rstd[:, :T]), r=AR, w=["rstd"])
                for c in lru_chunks:
                    P.op("dve", lambda e, c=c, T=T: e.scalar_tensor_tensor(out=bA[:, c, :T], in0=bA[:, c, :T], scalar=wnl[:, c:c + 1], in1=rstd[:, :T],
                                                                          op0=ALU.mult, op1=ALU.mult),
                         r=["rstd", "spt", "bB"] + AR, w=["bA"])

                def ev_res(n, ps, psn, T=T):
                    P.op("dve", lambda e: e.tensor_tensor(out=xt[:, n, :T], in0=xt[:, n, :T], in1=ps[:, :T], op=ALU.add),
                         r=[psn, "xt"] + AR, a=["xt"])
                stream_mm([wb_out[l, g] for g in range(8)], f"wb_out{l}", bA, "bA", T, ev_res)
                for h4 in range(4):
                    P.op("act", lambda e, h4=h4, T=T: e.activation(out=bB[:, h4 * 8:(h4 + 1) * 8, :T], in_=xt[:, h4 * 8:(h4 + 1) * 8, :T], func=AF.Square),
                         r=["xt"] + AR, w=["bB"] if h4 == 0 else (), a=() if h4 == 0 else ["bB"])
                for c in range(32):
                    P.op("pe", lambda e, c=c, T=T: e.matmul(pss[4][:, :T], lhsT=ones_b[:], rhs=bB[:, c, :T], start=(c == 0), stop=(c == 31)),
                         r=["bB", "ones_b"] + AR, w=["ps4"] if c == 0 else (), a=() if c == 0 else ["ps4"])
                P.op("act", lambda e, T=T: e.activation(out=rstd[:, :T], in_=pss[4][:, :T], func=AF.Sqrt, bias=epsc[:, 0:1], scale=1.0 / D),
                     r=["ps4", "epsc"] + AR, w=["rstd"])
                P.op("dve", lambda e, T=T: e.reciprocal(out=rstd[:, :T], in_=rstd[:, :T]), r=AR, w=["rstd"])
                for c in range(32):
                    P.op("dve", lambda e, c=c, T=T: e.scalar_tensor_tensor(out=bB[:, c, :T], in0=xt[:, c, :T], scalar=wnm[:, c:c + 1], in1=rstd[:, :T],
                                                                          op0=ALU.mult, op1=ALU.mult),
                         r=["xt", "rstd", "spt"] + AR, w=["bB"] if c == 0 else (), a=() if c == 0 else ["bB"])
                for qq in range(4):
                    def ev_up(n, ps, psn, T=T):
                        P.op("act", lambda e: e.activation(out=rtmp[:, :T], in_=ps[:, :T], func=AF.Relu), r=[psn] + AR, w=["rtmp"])
                        P.op("dve", lambda e: e.tensor_tensor(out=bA[:, n, :T], in0=rtmp[:, :T], in1=rtmp[:, :T], op=ALU.mult),
                             r=["rtmp"] + AR, w=["bA"] if n == 0 else (), a=() if n == 0 else ["bA"])
                    stream_mm([wb_up[l, qq * 8 + g] for g in range(8)], f"wb_up{l}", bB, "bB", T, ev_up)
                    stream_mm([wb_dn[l, qq, g] for g in range(8)], f"wb_dn{l}", bA, "bA", T, ev_res)
                if final:
                    for h4 in range(4):
                        P.op("act", lambda e, h4=h4, T=T: e.activation(out=bB[:, h4 * 8:(h4 + 1) * 8, :T], in_=xt[:, h4 * 8:(h4 + 1) * 8, :T], func=AF.Square),
                             r=["xt"] + AR, w=["bB"] if h4 == 0 else (), a=() if h4 == 0 else ["bB"])
                    for c in range(32):
                        P.op("pe", lambda e, c=c, T=T: e.matmul(pss[4][:, :T], lhsT=ones_b[:], rhs=bB[:, c, :T], start=(c == 0), stop=(c == 31)),
                             r=["bB", "ones_b"] + AR, w=["ps4"] if c == 0 else (), a=() if c == 0 else ["ps4"])
                    P.op("act", lambda e, T=T: e.activation(out=rstd[:, :T], in_=pss[4][:, :T], func=AF.Sqrt, bias=epsc[:, 0:1], scale=1.0 / D),
                         r=["ps4", "epsc"] + AR, w=["rstd"])
                    P.op("dve", lambda e, T=T: e.reciprocal(out=rstd[:, :T], in_=rstd[:, :T]), r=AR, w=["rstd"])
                    for c in range(32):
                        P.op("dve", lambda e, c=c, T=T: e.scalar_tensor_tensor(out=xt[:, c, :T], in0=xt[:, c, :T], scalar=wnf[:, c:c + 1], in1=rstd[:, :T],
                                                                              op0=ALU.mult, op1=ALU.mult),
                             r=["rstd", "spt", "bB"] + AR, w=["xt"])
                P.dma("sp", dstv[:, :, t0:t0 + T], xt[:, :, :T], r=["xt"] + AR, a=[dst_name])

        class Bump:
            def __init__(self):
                self.off = 0

            def f(self, shape):
                n = int(np.prod(shape[1:]))
                ap = f32v(self.off, shape)
                self.off += n
                return ap

            def b(self, shape):
                n = int(np.prod(shape[1:]))
                ap = bf16v(self.off, shape)
                self.off += (n + 1) // 2
                return ap

        bp = Bump()
        qkvr = bp.f([128, 6, 515]); zr = bp.f([128, 2, 512]); xlr = bp.f([128, 2, 515]); ylr = bp.f([128, 2, 512])
        gr = bp.f([128, 512]); gs = bp.f([128, 512])
        cqkv = bp.f([128, 6, 512]); sqt = bp.f([128, 512]); rs = bp.f([128, 512])
        qnT = bp.b([128, 8, 2, 64]); knT = bp.b([128, 8, 2, 64]); vT = bp.b([128, 8, 2, 64])
        sz = bp.f([128, 2, 512])
        Gb = bp.f([128, 8, 2, 64]); eGb = bp.f([128, 8, 2, 64]); Bb = bp.f([128, 8, 2, 64])
        nbs = bp.f([128, 8, 128]); tmpm = bp.f([128, 8, 128])
        Gcol = bp.f([128, 8]); Bcol = bp.f([128, 8]); GL = bp.f([128, 8]); ed = bp.f([128, 8]); eGc = bp.f([128, 8])
        sc1 = bp.f([128, 8]); nBcol = bp.f([128, 8]); sc2 = bp.f([128, 8, 2])
        tU = bp.f([128, 128]); DT = bp.f([128, 128]); DTn = bp.f([128, 128]); tL = bp.f([128, 128]); Dm = bp.f([128, 128])
        Ls = [bp.f([128, 128]) for _ in range(2)]; Us = [bp.f([128, 128]) for _ in range(2)]; Ps = [bp.f([128, 128]) for _ in range(2)]
        otm = bp.f([128, 128]); sqo = bp.f([128, 128]); on = bp.f([128, 128]); ss = bp.f([128, 2])
        TTb = bp.b([128, 128]); Kst = bp.b([128, 128]); Vb = bp.b([128, 128]); nKbg = bp.b([128, 128]); qkTb = bp.b([128, 128]); vnb = bp.b([128, 128])
        qgp = bp.b([128, 384]); nwp = bp.b([128, 384]); kdp = bp.b([128, 2, 128])
        mixd = bp.b([128, 2, 512]); mixl = bp.b([128, 2, 512])
        S = bp.f([128, 2, 128]); Sb = bp.b([128, 2, 128])
        xc = bp.f([128, 2, 512]); gA = bp.f([128, 512]); gI = bp.f([128, 512]); aT = bp.f([128, 512]); a2 = bp.f([128, 512])
        bT = bp.f([128, 512]); hs = bp.f([128, 512]); gy = bp.f([128, 512])
        wgt = bp.f([128, 4, 128]); hcar = bp.f([128, 2]); c1 = bp.f([128, 4]); negA = bp.f([128, 2])
        assert bp.off <= 44000, bp.off
        qgp_d = qgp.rearrange("p (a b) -> p a b", b=192)[:, :, 0:64]
        nwp_d = nwp.rearrange("p (a b) -> p a b", b=192)[:, :, 0:64]

        def phaseB(l):
            wcd = spv(l, O_WCD, 24); gc = spv(l, O_GC, 2); wcl = spv(l, O_WCL, 8); bcl = spv(l, O_BCL, 2)
            bga = spv(l, O_BGA, 2); bgx = spv(l, O_BGX, 2); lam = spv(l, O_LAM, 2); wnd = spv(l, O_WND, 128)
            B = lambda *names: list(names) + AR
            P.dma("sp", wgt[:], wg_in[l], r=AR, w=["wgt"])
            P.op("dve", lambda e: e.memset(gs[:], 0.0), r=AR, w=["gs"])
            P.op("dve", lambda e: e.memset(qgp[:], 0.0), r=AR, w=["qgp"])
            P.op("dve", lambda e: e.memset(nwp[:], 0.0), r=AR, w=["nwp"])
            P.op("act", lambda e: e.activation(out=c1[:, 0:2], in_=lam, func=AF.Exp, scale=-1.0), r=B("spt"), w=["c1"])
            P.op("act", lambda e: e.activation(out=c1[:, 0:2], in_=c1[:, 0:2], func=AF.Ln, bias=epsc[:, 1:2]), r=B("epsc"), w=["c1"])
            P.op("dve", lambda e: e.tensor_scalar(out=c1[:, 2:4], in0=c1[:, 0:2], scalar1=-16.0, scalar2=None, op0=ALU.mult), r=B("c1"), a=["c1"])
            P.op("dve", lambda e: e.tensor_scalar(out=c1[:, 0:2], in0=c1[:, 0:2], scalar1=-8.0, scalar2=None, op0=ALU.mult), r=AR, w=["c1"])
            P.op("act", lambda e: e.activation(out=negA[:, 0:1], in_=gc[:, 1:2], func=AF.Exp), r=B("spt"), w=["negA"])
            P.op("dve", lambda e: e.tensor_scalar(out=negA[:, 0:1], in0=negA[:, 0:1], scalar1=-1.0, scalar2=None, op0=ALU.mult), r=AR, w=["negA"])

            pv6 = proj[0:768, :].rearrange("(j p) t -> p j t", p=128)
            pvz = proj[768:1024, :].rearrange("(j p) t -> p j t", p=128)
            pvx = proj[1024:1280, :].rearrange("(j p) t -> p j t", p=128)
            pvy = proj[1280:1536, :].rearrange("(j p) t -> p j t", p=128)
            pvg = proj[1536:1664, :]
            mlv_d = mix_loc[0:256, :].rearrange("(h p) t -> p h t", p=128)
            mlv_l = mix_loc[256:512, :].rearrange("(h p) t -> p h t", p=128)

            def segment(si, col0, Wd, first, last, halo):
                Wp = max(Wd, 64)
                nch = Wp // 64
                sample = Wd < 64
                P.dma("sp", qkvr[:, :, 3:3 + Wd], pv6[:, :, col0:col0 + Wd], r=B("proj"), w=["qkvr"])
                P.dma("sp", xlr[:, :, 3:3 + Wd], pvx[:, :, col0:col0 + Wd], r=B("proj"), w=["xlr"])
                if halo[0] == "zero":
                    P.op("dve", lambda e: e.memset(qkvr[:, :, 0:3], 0.0), r=AR, a=["qkvr"])
                    P.op("dve", lambda e: e.memset(xlr[:, :, 0:3], 0.0), r=AR, a=["xlr"])
                elif halo[0] == "col":
                    hc = halo[1]
                    P.dma("sp", qkvr[:, :, 0:3], pv6[:, :, hc:hc + 3], r=B("proj"), a=["qkvr"])
                    P.dma("sp", xlr[:, :, 0:3], pvx[:, :, hc:hc + 3], r=B("proj"), a=["xlr"])
                else:
                    sr = halo[1]
                    P.dma("sp", qkvr[:, :, 0:3], scd_in[l, sr], r=AR, a=["qkvr"])
                    P.dma("sp", xlr[:, :, 0:3], scl_in[l, sr], r=AR, a=["xlr"])
                P.dma("sp", zr[:, :, :Wd], pvz[:, :, col0:col0 + Wd], r=B("proj"), w=["zr"])
                P.dma("sp", ylr[:, :, :Wd], pvy[:, :, col0:col0 + Wd], r=B("proj"), w=["ylr"])
                if sample:
                    P.op("dve", lambda e: e.memset(gr[:, 0:64], 0.0), r=AR, w=["gr"])
                    for t_, nm in ((qnT, "qnT"), (knT, "knT"), (vT, "vT")):
                        P.op("dve", lambda e, t_=t_: e.memset(t_[:, 0], 0.0), r=AR, w=[nm])
                    P.dma("sp", gr[:, :Wd], pvg[:, col0:col0 + Wd], r=B("proj", "gr"), a=["gr"])
                else:
                    P.dma("sp", gr[:, :Wd], pvg[:, col0:col0 + Wd], r=B("proj"), w=["gr"])
                if first:
                    if sample:
                        sr = halo[1]
                        P.dma("sp", S[:], sd_in[l, sr], r=AR, w=["S"])
                        P.dma("sp", hcar[:], sl_in[l, sr], r=AR, w=["hcar"])
                    else:
                        P.op("dve", lambda e: e.memset(S[:], 0.0), r=AR, w=["S"])
                        P.op("dve", lambda e: e.memset(hcar[:], 0.0), r=AR, w=["hcar"])
                    P.op("act", lambda e: e.activation(out=Sb[:], in_=S[:], func=AF.Copy), r=B("S"), w=["Sb"])

                for j6 in range(6):
                    P.op("dve", lambda e, j6=j6: e.tensor_scalar(out=cqkv[:, j6, :Wd], in0=qkvr[:, j6, 0:Wd], scalar1=wcd[:, j6 * 4:j6 * 4 + 1], scalar2=None, op0=ALU.mult),
                         r=B("qkvr", "spt"), w=["cqkv"] if j6 == 0 else (), a=() if j6 == 0 else ["cqkv"])
                    for j in range(1, 4):
                        P.op("dve", lambda e, j6=j6, j=j: e.scalar_tensor_tensor(out=cqkv[:, j6, :Wd], in0=qkvr[:, j6, j:j + Wd], scalar=wcd[:, j6 * 4 + j:j6 * 4 + j + 1],
                                                                               in1=cqkv[:, j6, :Wd], op0=ALU.mult, op1=ALU.add),
                             r=B("qkvr", "spt"), w=["cqkv"])
                P.op("act", lambda e: e.activation(out=cqkv[:, :, :Wd], in_=cqkv[:, :, :Wd], func=AF.Silu), r=AR, w=["cqkv"])
                for j6 in range(4):
                    h = j6 % 2
                    dst, dn = (qnT, "qnT") if j6 < 2 else (knT, "knT")
                    P.op("act", lambda e, j6=j6: e.activation(out=sqt[:, :Wd], in_=cqkv[:, j6, :Wd], func=AF.Square), r=B("cqkv"), w=["sqt"])
                    P.op("pe", lambda e: e.matmul(pss[0][:, :Wd], lhsT=ones_f, rhs=sqt[:, :Wd], start=True, stop=True), r=B("sqt", "cst"), w=["ps0"])
                    P.op("act", lambda e: e.activation(out=rs[:, :Wd], in_=pss[0][:, :Wd], func=AF.Sqrt, bias=epsc[:, 0:1], scale=1.0), r=B("ps0", "epsc"), w=["rs"])
                    P.op("dve", lambda e: e.reciprocal(out=rs[:, :Wd], in_=rs[:, :Wd]), r=AR, w=["rs"])
                    if sample:
                        o_ap = dst[:, 0, h, 0:Wd]; i0 = cqkv[:, j6, :Wd]; i1 = rs[:, :Wd]
                    else:
                        o_ap = dst[:, :, h, :]
                        i0 = cqkv[:, j6, :].rearrange("p (c t) -> p c t", t=64)
                        i1 = rs[:, :].rearrange("p (c t) -> p c t", t=64)
                    scl = (128.0 ** -0.5) if j6 < 2 else 1.0
                    P.op("dve", lambda e, o_ap=o_ap, i0=i0, i1=i1, scl=scl: e.scalar_tensor_tensor(out=o_ap, in0=i0, scalar=scl, in1=i1, op0=ALU.mult, op1=ALU.mult),
                         r=B("cqkv", "rs", dn), a=[dn])
                for h in range(2):
                    if sample:
                        o_ap = vT[:, 0, h, 0:Wd]; i0 = cqkv[:, 4 + h, :Wd]
                    else:
                        o_ap = vT[:, :, h, :]; i0 = cqkv[:, 4 + h, :].rearrange("p (c t) -> p c t", t=64)
                    P.op("dve", lambda e, o_ap=o_ap, i0=i0: e.tensor_copy(out=o_ap, in_=i0), r=B("cqkv", "vT"), a=["vT"])
                P.op("act", lambda e: e.activation(out=sz[:, :, :Wd], in_=zr[:, :, :Wd], func=AF.Silu), r=B("zr"), w=["sz"])
                for p0 in (0, 32):
                    P.op("act", lambda e, p0=p0: e.activation(out=gr[p0:p0 + 1, :Wd], in_=gr[p0:p0 + 1, :Wd], func=AF.Sigmoid), r=B("gr"), a=["gr"])
                for p0 in (64, 96):
                    P.op("act", lambda e, p0=p0: e.activation(out=gr[p0:p0 + 1, :Wd], in_=gr[p0:p0 + 1, :Wd], func=AF.Exp, bias=gc[p0:p0 + 1, 0:1]), r=B("spt", "gr"), a=["gr"])
                for p0 in (64, 96):
                    P.op("act", lambda e, p0=p0: e.activation(out=gr[p0:p0 + 1, :Wd], in_=gr[p0:p0 + 1, :Wd], func=AF.Ln, bias=epsc[p0:p0 + 1, 1:2]), r=B("epsc", "gr"), a=["gr"])
                for p0 in (64, 96):
                    P.op("dve", lambda e, p0=p0: e.tensor_scalar(out=gr[p0:p0 + 1, :Wd], in0=gr[p0:p0 + 1, :Wd], scalar1=negA[p0:p0 + 1, 0:1], scalar2=None, op0=ALU.mult),
                         r=B("negA"), w=["gr"])
                for p0 in (64, 96):
                    P.op("dve", lambda e, p0=p0: e.tensor_tensor_scan(out=gs[p0:p0 + 1, :Wp], data0=mrow[p0:p0 + 1, :Wp], data1=gr[p0:p0 + 1, :Wp], initial=0.0,
                                                                     op0=ALU.mult, op1=ALU.add), r=B("gr", "cst", "gs"), a=["gs"])
                for h in range(2):
                    p0 = 64 + 32 * h
                    P.op("pe", lambda e, p0=p0: e.matmul(pss[1][:, :Wp], lhsT=cst[:, C_SEL + (p0 // 32) * 128:C_SEL + (p0 // 32 + 1) * 128], rhs=gs[:, :Wp], start=True, stop=True), r=B("gs", "cst"), w=["ps1"])
                    P.op("act", lambda e, h=h: e.activation(out=Gb[:, :nch, h, :], in_=pss[1][:, :Wp].rearrange("p (c t) -> p c t", t=64), func=AF.Copy),
                         r=B("ps1"), w=["Gb"] if h == 0 else (), a=() if h == 0 else ["Gb"])
                for h in range(2):
                    p0 = 32 * h
                    P.op("pe", lambda e, p0=p0: e.matmul(pss[1][:, :Wp], lhsT=cst[:, C_SEL + (p0 // 32) * 128:C_SEL + (p0 // 32 + 1) * 128], rhs=gr[:, :Wp], start=True, stop=True), r=B("gr", "cst"), w=["ps1"])
                    P.op("act", lambda e, h=h: e.activation(out=Bb[:, :nch, h, :], in_=pss[1][:, :Wp].rearrange("p (c t) -> p c t", t=64), func=AF.Copy),
                         r=B("ps1"), w=["Bb"] if h == 0 else (), a=() if h == 0 else ["Bb"])
                Gb3 = Gb.rearrange("p c h t -> p c (h t)")
                Bb3 = Bb.rearrange("p c h t -> p c (h t)")
                eGb3 = eGb.rearrange("p c h t -> p c (h t)")
                P.op("act", lambda e: e.activation(out=eGb3[:, :nch, :], in_=Gb3[:, :nch, :], func=AF.Exp), r=B("Gb"), w=["eGb"])
                P.op("dve", lambda e: e.scalar_tensor_tensor(out=nbs[:, :nch, :], in0=Bb3[:, :nch, :], scalar=-1.0, in1=strictU.unsqueeze(1).broadcast_to([128, nch, 128]),
                                                             op0=ALU.mult, op1=ALU.mult), r=B("Bb", "cst"), w=["nbs"])
                idb = ident_f.unsqueeze(1).broadcast_to([128, nch, 128])
                P.op("dve", lambda e: e.tensor_tensor(out=tmpm[:, :nch, :], in0=Gb3[:, :nch, :], in1=idb, op=ALU.mult), r=B("Gb", "cst"), w=["tmpm"])
                P.op("dve", lambda e: e.tensor_reduce(out=Gcol[:, :nch], in_=tmpm[:, :nch, :], axis=AX.X, op=ALU.add), r=B("tmpm"), w=["Gcol"])
                P.op("dve", lambda e: e.tensor_tensor(out=tmpm[:, :nch, :], in0=Bb3[:, :nch, :], in1=idb, op=ALU.mult), r=B("Bb", "cst"), w=["tmpm"])
                P.op("dve", lambda e: e.tensor_reduce(out=Bcol[:, :nch], in_=tmpm[:, :nch, :], axis=AX.X, op=ALU.add), r=B("tmpm"), w=["Bcol"])
                P.op("dve", lambda e: e.tensor_copy(out=GL[0:64, :nch], in_=Gb[0:64, :nch, 0, 63]), r=B("Gb"), w=["GL"])
                P.op("dve", lambda e: e.tensor_copy(out=GL[64:128, :nch], in_=Gb[64:128, :nch, 1, 63]), r=B("Gb"), a=["GL"])
                P.op("dve", lambda e: e.tensor_tensor(out=ed[:, :nch], in0=GL[:, :nch], in1=Gcol[:, :nch], op=ALU.subtract), r=B("GL", "Gcol"), w=["ed"])
                P.op("act", lambda e: e.activation(out=ed[:, :nch], in_=ed[:, :nch], func=AF.Exp), r=AR, w=["ed"])
                P.op("act", lambda e: e.activation(out=eGc[:, :nch], in_=Gcol[:, :nch], func=AF.Exp), r=B("Gcol"), w=["eGc"])
                P.op("dve", lambda e: e.scalar_tensor_tensor(out=sc1[:, :nch], in0=Bcol[:, :nch], scalar=-1.0, in1=eGc[:, :nch], op0=ALU.mult, op1=ALU.mult),
                     r=B("Bcol", "eGc"), w=["sc1"])
                P.op("dve", lambda e: e.tensor_scalar(out=nBcol[:, :nch], in0=Bcol[:, :nch], scalar1=-1.0, scalar2=None, op0=ALU.mult), r=B("Bcol"), w=["nBcol"])
                for h in range(2):
                    P.op("dve", lambda e, h=h: e.tensor_scalar(out=sc2[:, :nch, h], in0=ed[:, :nch], scalar1=hmask[:, h:h + 1], scalar2=None, op0=ALU.mult),
                         r=B("ed", "cst"), w=["sc2"] if h == 0 else (), a=() if h == 0 else ["sc2"])

                for c in range(nch):
                    kn_c = knT[:, c].rearrange("p h t -> p (h t)")
                    qn_c = qnT[:, c].rearrange("p h t -> p (h t)")
                    v_c = vT[:, c].rearrange("p h t -> p (h t)")
                    Gb_c = Gb3[:, c, :]
                    P.op("pe", lambda e, kn_c=kn_c: e.matmul(pss[4][:, 0:128], lhsT=kn_c, rhs=kn_c, start=True, stop=True), r=B("knT"), w=["ps4"])
                    P.op("pe", lambda e, kn_c=kn_c, qn_c=qn_c: e.matmul(pss[4][:, 128:256], lhsT=kn_c, rhs=qn_c, start=True, stop=True), r=B("knT", "qnT"), w=["ps4"])
                    P.op("pe", lambda e, kn_c=kn_c: e.matmul(pb[:, 0:128], lhsT=kn_c, rhs=ident_b[:], start=True, stop=True), r=B("knT", "ident_b"), w=["pb"])
                    P.op("pe", lambda e, v_c=v_c: e.matmul(pb[:, 128:256], lhsT=v_c, rhs=ident_b[:], start=True, stop=True), r=B("vT", "ident_b"), w=["pb"])
                    P.op("dve", lambda e, c=c, Gb_c=Gb_c: e.scalar_tensor_tensor(out=tU[:], in0=Gb_c, scalar=Gcol[:, c:c + 1], in1=negU, op0=ALU.subtract, op1=ALU.add),
                         r=B("Gb", "Gcol", "cst"), w=["tU"])
                    P.op("act", lambda e: e.activation(out=DT[:], in_=tU[:], func=AF.Exp), r=B("tU"), w=["DT"])
                    P.op("dve", lambda e, c=c: e.tensor_tensor(out=DTn[:], in0=DT[:], in1=nbs[:, c, :], op=ALU.mult), r=B("DT", "nbs"), w=["DTn"])
                    P.op("dve", lambda e: e.tensor_tensor(out=Us[0][:], in0=pss[4][:, 0:128], in1=DTn[:], op=ALU.mult), r=B("ps4", "DTn"), w=["U0"])
                    P.op("dve", lambda e, c=c, Gb_c=Gb_c: e.scalar_tensor_tensor(out=tL[:], in0=Gb_c, scalar=Gcol[:, c:c + 1], in1=posL, op0=ALU.subtract, op1=ALU.add),
                         r=B("Gb", "Gcol", "cst"), w=["tL"])
                    P.op("act", lambda e: e.activation(out=Dm[:], in_=tL[:], func=AF.Exp, scale=-1.0), r=B("tL"), w=["Dm"])
                    P.op("dve", lambda e, c=c: e.scalar_tensor_tensor(out=Ls[0][:], in0=pss[4][:, 0:128], scalar=nBcol[:, c:c + 1], in1=Dm[:], op0=ALU.mult, op1=ALU.mult),
                         r=B("ps4", "nBcol", "Dm"), w=["L0"])
                    P.op("dve", lambda e: e.tensor_tensor(out=qkTb[:], in0=pss[4][:, 128:256], in1=DT[:], op=ALU.mult), r=B("ps4", "DT"), w=["qkTb"])
                    P.op("dve", lambda e: e.tensor_tensor(out=Ps[0][:], in0=Us[0][:], in1=ident_f, op=ALU.add), r=B("U0", "cst"), w=["P0"])
                    cur = 0
                    for k in range(1, 6):
                        nxt = 1 - cur
                        P.op("pe", lambda e, cur=cur: e.matmul(pss[5][:, 0:128], lhsT=Us[cur][:], rhs=Ls[cur][:], start=True, stop=True),
                             r=B(f"U{cur}", f"L{cur}"), w=["ps5"])
                        if k < 5:
                            P.op("pe", lambda e, cur=cur: e.matmul(pss[5][:, 128:256], lhsT=Ls[cur][:], rhs=Us[cur][:], start=True, stop=True),
                                 r=B(f"U{cur}", f"L{cur}"), w=["ps5"])
                        P.op("act", lambda e, nxt=nxt: e.activation(out=Ls[nxt][:], in_=pss[5][:, 0:128], func=AF.Copy), r=B("ps5"), w=[f"L{nxt}"])
                        if k < 5:
                            P.op("act", lambda e, nxt=nxt: e.activation(out=Us[nxt][:], in_=pss[5][:, 128:256], func=AF.Copy), r=B("ps5"), w=[f"U{nxt}"])
                        P.op("pe", lambda e, cur=cur, nxt=nxt: e.matmul(pss[5][:, 256:384], lhsT=Ls[nxt][:], rhs=Ps[cur][:], start=True, stop=True),
                             r=B(f"L{nxt}", f"P{cur}"), w=["ps5"])
                        P.op("dve", lambda e, cur=cur, nxt=nxt: e.tensor_tensor(out=Ps[nxt][:], in0=Ps[cur][:], in1=pss[5][:, 256:384], op=ALU.add),
                             r=B("ps5", f"P{cur}"), w=[f"P{nxt}"])
                        cur = nxt
                    P.op("act", lambda e, cur=cur: e.activation(out=TTb[:], in_=Ps[cur][:], func=AF.Copy), r=B(f"P{cur}"), w=["TTb"])
                    P.op("act", lambda e: e.activation(out=Kst[:], in_=pb[:, 0:128], func=AF.Copy), r=B("pb"), w=["Kst"])
                    P.op("dve", lambda e, c=c: e.tensor_scalar(out=nKbg[:], in0=Kst[:], scalar1=sc1[:, c:c + 1], scalar2=None, op0=ALU.mult), r=B("Kst", "sc1"), w=["nKbg"])
                    P.op("dve", lambda e, c=c: e.tensor_scalar(out=Vb[:], in0=pb[:, 128:256], scalar1=Bcol[:, c:c + 1], scalar2=None, op0=ALU.mult), r=B("pb", "Bcol"), w=["Vb"])
                    for h in range(2):
                        P.op("dve", lambda e, c=c, h=h: e.tensor_scalar(out=kdp[:, h, :], in0=Kst[:], scalar1=sc2[:, c, h:h + 1], scalar2=None, op0=ALU.mult),
                             r=B("Kst", "sc2"), w=["kdp"] if h == 0 else (), a=() if h == 0 else ["kdp"])
                    P.op("dve", lambda e, c=c: e.tensor_tensor(out=qgp_d, in0=qnT[:, c], in1=eGb[:, c], op=ALU.mult), r=B("qnT", "eGb"), w=["qgp"])
                    P.op("pe", lambda e: e.matmul(pss[6][:, 0:128], lhsT=nKbg[:], rhs=TTb[:], start=True, stop=True), r=B("nKbg", "TTb"), w=["ps6"])
                    P.op("act", lambda e: e.activation(out=nwp_d, in_=pss[6][:, 0:128].rearrange("p (h t) -> p h t", t=64), func=AF.Copy), r=B("ps6"), w=["nwp"])
                    P.op("pe", lambda e: e.matmul(pss[6][:, 128:256], lhsT=TTb[:], rhs=Vb[:], start=True, stop=False), r=B("TTb", "Vb"), w=["ps6"])
                    P.op("pe", lambda e: e.matmul(pss[6][:, 128:256], lhsT=nwp[:, 0:128], rhs=Sb[:, 0, :], start=False, stop=False), r=B("nwp", "Sb"), a=["ps6"])
                    P.op("pe", lambda e: e.matmul(pss[6][:, 128:256], lhsT=nwp[:, 128:256], rhs=Sb[:, 1, :], start=False, stop=True), r=B("nwp", "Sb"), a=["ps6"])
                    P.op("act", lambda e: e.activation(out=vnb[:], in_=pss[6][:, 128:256], func=AF.Copy), r=B("ps6"), w=["vnb"])
                    P.op("pe", lambda e: e.matmul(pss[6][:, 256:384], lhsT=qgp[:, 0:128], rhs=Sb[:, 0, :], start=True, stop=False), r=B("qgp", "Sb"), w=["ps6"])
                    P.op("pe", lambda e: e.matmul(pss[6][:, 256:384], lhsT=qgp[:, 128:256], rhs=Sb[:, 1, :], start=False, stop=False), r=B("qgp", "Sb"), a=["ps6"])
                    P.op("pe", lambda e: e.matmul(pss[6][:, 256:384], lhsT=qkTb[:], rhs=vnb[:], start=False, stop=True), r=B("qkTb", "vnb"), a=["ps6"])
                    for h in range(2):
                        P.op("pe", lambda e, h=h: e.matmul(pss[3][:, h * 128:(h + 1) * 128], lhsT=kdp[:, h, :], rhs=vnb[:], start=True, stop=True),
                             r=B("kdp", "vnb"), w=["ps3"])
                    for h in range(2):
                        P.op("dve", lambda e, c=c, h=h: e.scalar_tensor_tensor(out=S[:, h, :], in0=S[:, h, :], scalar=eGb[:, c, h, 63:64], in1=pss[3][:, h * 128:(h + 1) * 128],
                                                                             op0=ALU.mult, op1=ALU.add), r=B("ps3", "eGb", "Sb"), w=["S"] if h == 0 else (), a=() if h == 0 else ["S"])
                    P.op("act", lambda e: e.activation(out=Sb[:], in_=S[:], func=AF.Copy), r=B("S"), w=["Sb"])
                    P.op("act", lambda e: e.activation(out=otm[:], in_=pss[6][:, 256:384], func=AF.Copy), r=B("ps6"), w=["otm"])
                    P.op("act", lambda e: e.activation(out=sqo[:], in_=otm[:], func=AF.Square), r=B("otm"), w=["sqo"])
                    P.op("dve", lambda e: e.tensor_reduce(out=ss[:, 0:1], in_=sqo[:], axis=AX.X, op=ALU.add), r=B("sqo"), w=["ss"])
                    P.op("act", lambda e: e.activation(out=ss[:, 0:1], in_=ss[:, 0:1], func=AF.Sqrt, bias=epsc[:, 0:1], scale=1.0 / 128), r=B("epsc"), w=["ss"])
                    P.op("dve", lambda e: e.reciprocal(out=ss[:, 0:1], in_=ss[:, 0:1]), r=AR, w=["ss"])
                    P.op("dve", lambda e: e.scalar_tensor_tensor(out=on[:], in0=otm[:], scalar=ss[:, 0:1], in1=wnd, op0=ALU.mult, op1=ALU.mult), r=B("otm", "ss", "spt"), w=["on"])
                    P.op("pe", lambda e: e.matmul(pss[3][:, 256:384], lhsT=on[:], rhs=ident_f, start=True, stop=True), r=B("on", "cst"), w=["ps3"])
                    wcols = min(Wd, 64)
                    P.op("dve", lambda e, c=c, wcols=wcols: e.tensor_tensor(out=mixd[:, :, c * 64:c * 64 + wcols],
                                                                           in0=pss[3][:, 256:384].rearrange("p (h t) -> p h t", t=64)[:, :, 0:wcols],
                                                                           in1=sz[:, :, c * 64:c * 64 + wcols], op=ALU.mult),
                         r=B("ps3", "sz"), w=["mixd"] if c == 0 else (), a=() if c == 0 else ["mixd"])

                for g in range(2):
                    P.op("dve", lambda e, g=g: e.tensor_scalar(out=xc[:, g, :Wd], in0=xlr[:, g, 0:Wd], scalar1=wcl[:, g * 4:g * 4 + 1], scalar2=bcl[:, g:g + 1],
                                                              op0=ALU.mult, op1=ALU.add), r=B("xlr", "spt"), w=["xc"])
                    for j in range(1, 4):
                        P.op("dve", lambda e, g=g, j=j: e.scalar_tensor_tensor(out=xc[:, g, :Wd], in0=xlr[:, g, j:j + Wd], scalar=wcl[:, g * 4 + j:g * 4 + j + 1],
                                                                             in1=xc[:, g, :Wd], op0=ALU.mult, op1=ALU.add), r=B("xlr", "spt"), w=["xc"])
                    P.op("pe", lambda e, g=g: e.matmul(pss[0][:, :Wd], lhsT=wgt[:, g, :], rhs=xc[:, g, :Wd], start=True, stop=True), r=B("wgt", "xc"), w=["ps0"])
                    P.op("act", lambda e, g=g: e.activation(out=gA[:, :Wd], in_=pss[0][:, :Wd], func=AF.Sigmoid, bias=bga[:, g:g + 1]), r=B("ps0", "spt"), w=["gA"])
                    P.op("pe", lambda e, g=g: e.matmul(pss[0][:, :Wd], lhsT=wgt[:, 2 + g, :], rhs=xc[:, g, :Wd], start=True, stop=True), r=B("wgt", "xc"), w=["ps0"])
                    P.op("act", lambda e, g=g: e.activation(out=gI[:, :Wd], in_=pss[0][:, :Wd], func=AF.Sigmoid, bias=bgx[:, g:g + 1]), r=B("ps0", "spt"), w=["gI"])
                    P.op("act", lambda e, g=g: e.activation(out=aT[:, :Wd], in_=gA[:, :Wd], func=AF.Exp, scale=c1[:, g:g + 1]), r=B("gA", "c1"), w=["aT"])
                    P.op("act", lambda e, g=g: e.activation(out=a2[:, :Wd], in_=gA[:, :Wd], func=AF.Exp, scale=c1[:, 2 + g:3 + g]), r=B("gA", "c1"), w=["a2"])
                    P.op("act", lambda e: e.activation(out=a2[:, :Wd], in_=a2[:, :Wd], func=AF.Sqrt, scale=-1.0, bias=epsc[:, 1:2]), r=B("epsc"), w=["a2"])
                    P.op("dve", lambda e: e.tensor_tensor(out=bT[:, :Wd], in0=a2[:, :Wd], in1=gI[:, :Wd], op=ALU.mult), r=B("a2", "gI"), w=["bT"])
                    P.op("dve", lambda e, g=g: e.tensor_tensor(out=bT[:, :Wd], in0=bT[:, :Wd], in1=xc[:, g, :Wd], op=ALU.mult), r=B("xc"), w=["bT"])
                    P.op("dve", lambda e, g=g: e.tensor_tensor_scan(out=hs[:, :Wd], data0=aT[:, :Wd], data1=bT[:, :Wd], initial=hcar[:, g:g + 1], op0=ALU.mult, op1=ALU.add),
                         r=B("aT", "bT", "hcar"), w=["hs"])
                    P.op("dve", lambda e, g=g: e.tensor_copy(out=hcar[:, g:g + 1], in_=hs[:, Wd - 1:Wd]), r=B("hs"), w=["hcar"])
                    P.op("act", lambda e, g=g: e.activation(out=gy[:, :Wd], in_=ylr[:, g, :Wd], func=AF.Gelu), r=B("ylr"), w=["gy"])
                    P.op("dve", lambda e, g=g: e.tensor_tensor(out=mixl[:, g, :Wd], in0=gy[:, :Wd], in1=hs[:, :Wd], op=ALU.mult), r=B("gy", "hs"),
                         w=["mixl"] if g == 0 else (), a=() if g == 0 else ["mixl"])
                P.dma("sp", mlv_d[:, :, col0:col0 + Wd], mixd[:, :, :Wd], r=B("mixd"), a=["mix_loc"])
                P.dma("sp", mlv_l[:, :, col0:col0 + Wd], mixl[:, :, :Wd], r=B("mixl"), a=["mix_loc"])
                if last:
                    P.dma("sp", o_sd[l, si], S[:], r=B("S"), a=["o_sd"])
                    P.dma("sp", o_cd[l, si], qkvr[:, :, Wd:Wd + 3], r=B("qkvr"), a=["o_cd"])
                    P.dma("sp", o_l[l, si], hcar[:], r=B("hcar"), a=["o_l"])
                    P.dma("sp", o_cl[l, si], xlr[:, :, Wd:Wd + 3], r=B("xlr"), a=["o_cl"])

            if debug == "Bsim":
                segment(0, 0, 512, first=True, last=False, halo=("zero",))
                segment(0, 512, 512, first=False, last=True, halo=("col", 509))
                segment(2, 2048, 16, first=True, last=True, halo=("state", 0))
                return
            for b in range(2):
                for j in range(4):
                    r = 4 * b + j
                    for s4 in range(4):
                        col0 = r * TOK + s4 * 512
                        if j == 0 and s4 == 0:
                            halo = ("zero",)
                        elif s4 == 0:
                            halo = ("col", (r - 1) * TOK + 2045)
                        else:
                            halo = ("col", col0 - 3)
                        segment(b, col0, 512, first=(j == 0 and s4 == 0), last=(j == 3 and s4 == 3), halo=halo)
            for r in range(NCORE):
                segment(2 + r, r * TOK + 2048, 16, first=True, last=True, halo=("state", r))

        cur_ap, cur_name = xT_in, "xT_in"
        for l in range(DEPTH):
            cast_win_dram(l)
        outs_final = ["yT", "o_sd", "o_cd", "o_l", "o_cl"]
        if debug == "Bsim":
            phaseB(0)
            P.dma("sp", mix_dbg[:, 0:1024], mix_loc[:, 0:1024], r=["mix_loc"], w=["mix_dbg"])
            P.dma("sp", mix_dbg[:, 2048:2064], mix_loc[:, 2048:2064], r=["mix_loc"], a=["mix_dbg"])
            outs_final = ["mix_dbg", "o_sd", "o_cd", "o_l", "o_cl"]
        for l in range(DEPTH if debug != "Bsim" else 0):
            phaseA(l, cur_ap, cur_name)
            if debug in ("A", "AG"):
                P.dma("sp", xn_dbg[:, :], xn_loc[:, :], r=["xn_loc"], w=["xn_dbg"])
                outs_final = ["xn_dbg"]
                if debug == "A":
                    break
            P.allgather(xn_loc, xn_all, "xn_loc", "xn_all")
            if debug == "AG":
                P.dma("sp", xa_dbg[:, :], xn_all[:, 0:64], r=["xn_all"], w=["xa_dbg"])
                outs_final = ["xn_dbg", "xa_dbg"]
                break
            if big:
                cast_dense(l)
            barrier()
            cast_win(l)
            phaseA2(l)
            barrier()
            if debug == "A2":
                outs_final = ["proj"]
                break
            phaseB(l)
            if debug == "B":
                P.dma("sp", mix_dbg[:, :], mix_loc[:, :], r=["mix_loc"], w=["mix_dbg"])
                outs_final = ["proj", "mix_dbg", "o_sd", "o_cd", "o_l", "o_cl"]
                break
            P.allgather(mix_loc, mix_all, "mix_loc", "mix_all")
            barrier()
            final = (l == DEPTH - 1)
            dst_ap, dst_name = (yT, "yT") if final else (xT1, "xT1")
            phaseC(l, cur_ap, cur_name, dst_ap, dst_name, final)
            barrier()
            cur_ap, cur_name = dst_ap, dst_name
        tokf = P.nop("sp", r=outs_final)
        P._wait("sp", tokf)

        with nc.Block() as block:
            @block.sync
            def _(eng):
                for f in P.q["sp"]:
                    f(eng)

            @block.scalar
            def _(eng):
                for f in P.q["act"]:
                    f(eng)

            @block.vector
            def _(eng):
                for f in P.q["dve"]:
                    f(eng)

            @block.tensor
            def _(eng):
                for f in P.q["pe"]:
                    f(eng)

            @block.gpsimd
            def _(eng):
                for f in P.q["pool"]:
                    f(eng)
    return nc


def _consts():
    c = np.zeros((128, C_TOT), np.float32)
    c[:, C_ID:C_ID + 128] = np.eye(128, dtype=np.float32)
    p = np.arange(128)
    hp, jp = p // 64, p % 64
    same = hp[:, None] == hp[None, :]
    c[:, C_NEGU:C_NEGU + 128] = np.where(same & (jp[None, :] >= jp[:, None]), 0.0, -1e9)
    c[:, C_POSL:C_POSL + 128] = np.where(same & (jp[None, :] < jp[:, None]), 0.0, 1e9)
    c[:, C_SU:C_SU + 128] = np.where(same & (jp[None, :] > jp[:, None]), 1.0, 0.0)
    c[:, C_HM] = (hp == 0)
    c[:, C_HM + 1] = (hp == 1)
    m = np.ones(512, np.float32)
    m[::64] = 0.0
    c[:, C_MROW:C_MROW + 512] = m[None, :]
    c[:, C_ONES:C_ONES + 128] = 1.0
    for idx, p0 in enumerate((0, 32, 64, 96)):
        c[p0, C_SEL + idx * 128:C_SEL + (idx + 1) * 128] = 1.0
    return c


_NC_CACHE = {}


def kernel(x_prompt, x_sample, state_delta, state_conv_delta, state_lru, state_conv_lru,
           w_norm_mix, w_in, w_conv_delta, a_log, dt_bias, w_norm_delta,
           w_conv_lru, b_conv_lru, w_gate_a, b_gate_a, w_gate_x, b_gate_x,
           lam, w_norm_lru, w_out, w_norm_mlp, w_mlp_up, w_mlp_down, w_norm_final):
    f = lambda a: np.asarray(a, dtype=np.float32)
    x_prompt, x_sample = f(x_prompt), f(x_sample)
    w_in, w_out, w_mlp_up, w_mlp_down = f(w_in), f(w_out), f(w_mlp_up), f(w_mlp_down)
    cst = _consts()
    perm = np.concatenate([np.concatenate([np.arange(c * 256, (c + 1) * 256), 2048 + np.arange(c * 256, (c + 1) * 256)]) for c in range(NCORE)])
    w_out_p = np.ascontiguousarray(w_out[:, perm, :])
    to_cols = lambda v: np.ascontiguousarray(f(v).reshape(32, 128).T)
    in_maps = []
    for c in range(NCORE):
        b, j = c // 4, c % 4
        xT = np.empty((D, TOK), np.float32)
        xT[:, :2048] = x_prompt[b, j * 2048:(j + 1) * 2048, :].T
        xT[:, 2048:] = x_sample[c].T
        wi = np.zeros((DEPTH, D, NW), np.float32)
        for h in range(2):
            H = 2 * c + h
            wi[:, :, (0 + h) * 128:(1 + h) * 128] = w_in[:, :, H * 128:(H + 1) * 128]
            wi[:, :, (2 + h) * 128:(3 + h) * 128] = w_in[:, :, 2048 + H * 128:2048 + (H + 1) * 128]
            wi[:, :, (4 + h) * 128:(5 + h) * 128] = w_in[:, :, 4096 + H * 128:4096 + (H + 1) * 128]
            wi[:, :, (6 + h) * 128:(7 + h) * 128] = w_in[:, :, 6144 + H * 128:6144 + (H + 1) * 128]
            wi[:, :, (8 + h) * 128:(9 + h) * 128] = w_in[:, :, 8224 + H * 128:8224 + (H + 1) * 128]
            wi[:, :, (10 + h) * 128:(11 + h) * 128] = w_in[:, :, 10272 + H * 128:10272 + (H + 1) * 128]
            wi[:, :, 1536 + 32 * h] = w_in[:, :, 8192 + H]
            wi[:, :, 1536 + 64 + 32 * h] = w_in[:, :, 8208 + H]
        sp = np.zeros((DEPTH, 128, NS), np.float32)
        wg = np.zeros((DEPTH, 128, 4, 128), np.float32)
        for l in range(DEPTH):
            sp[l, :, O_WNMIX:O_WNMIX + 32] = to_cols(w_norm_mix[l])
            sp[l, :, O_WNMLP:O_WNMLP + 32] = to_cols(w_norm_mlp[l])
            sp[l, :, O_WNF:O_WNF + 32] = to_cols(w_norm_final)
            wnl = np.ones((128, 32), np.float32)
            for s in range(NCORE):
                for g in range(2):
                    G = 2 * s + g
                    wnl[:, s * 4 + 2 + g] = f(w_norm_lru)[l, G * 128:(G + 1) * 128]
            sp[l, :, O_WNL:O_WNL + 32] = wnl
            for which in range(3):
                for h in range(2):
                    H = 2 * c + h
                    blk = f(w_conv_delta)[l, :, which * 2048 + H * 128:which * 2048 + (H + 1) * 128]
                    j6 = which * 2 + h
                    sp[l, :, O_WCD + j6 * 4:O_WCD + j6 * 4 + 4] = blk.T
            for h in range(2):
                H = 2 * c + h
                sp[l, 64 + 32 * h, O_GC] = f(dt_bias)[l, H]
                sp[l, 64 + 32 * h, O_GC + 1] = f(a_log)[l, H]
            for g in range(2):
                G = 2 * c + g
                sl_ = slice(G * 128, (G + 1) * 128)
                sp[l, :, O_WCL + g * 4:O_WCL + g * 4 + 4] = f(w_conv_lru)[l, :, sl_].T
                sp[l, :, O_BCL + g] = f(b_conv_lru)[l, sl_]
                sp[l, :, O_BGA + g] = f(b_gate_a)[l, sl_]
                sp[l, :, O_BGX + g] = f(b_gate_x)[l, sl_]
                sp[l, :, O_LAM + g] = f(lam)[l, sl_]
                wg[l, :, g, :] = f(w_gate_a)[l, G]
                wg[l, :, 2 + g, :] = f(w_gate_x)[l, G]
            sp[l, :, O_WND:O_WND + 128] = f(w_norm_delta)[l][None, :]
        sd = np.ascontiguousarray(f(state_delta)[:, :, 2 * c:2 * c + 2].transpose(0, 1, 3, 2, 4))
        scd_full = f(state_conv_delta).reshape(DEPTH, 8, 3, 3, 16, 128)
        scd = np.ascontiguousarray(scd_full[:, :, :, :, 2 * c:2 * c + 2, :].transpose(0, 1, 5, 3, 4, 2)).reshape(DEPTH, 8, 128, 6, 3)
        sl = np.ascontiguousarray(f(state_lru).reshape(DEPTH, 8, 16, 128)[:, :, 2 * c:2 * c + 2, :].transpose(0, 1, 3, 2))
        scl_full = f(state_conv_lru).reshape(DEPTH, 8, 3, 16, 128)
        scl = np.ascontiguousarray(scl_full[:, :, :, 2 * c:2 * c + 2, :].transpose(0, 1, 4, 3, 2))
        in_maps.append({"xT": xT, "w_in": wi, "w_out": w_out_p, "w_up": w_mlp_up, "w_dn": w_mlp_down,
                        "sp": sp, "wg": wg, "cst": cst, "sd": sd, "scd": scd, "sl": sl, "scl": scl})

    if "nc" not in _NC_CACHE:
        _NC_CACHE["nc"] = build_program()
    nc = _NC_CACHE["nc"]
    res = run_bass_kernel_spmd(nc, in_maps, core_ids=list(range(NCORE)))
    R = res.results

    y_prompt = np.empty((2, 8192, D), np.float32)
    y_sample = np.empty((8, 16, D), np.float32)
    d_st = np.empty((DEPTH, 10, 16, 128, 128), np.float32)
    c_d = np.empty((DEPTH, 10, 3, 3, 16, 128), np.float32)
    l_st = np.empty((DEPTH, 10, 16, 128), np.float32)
    c_l = np.empty((DEPTH, 10, 3, 16, 128), np.float32)
    for c in range(NCORE):
        b, j = c // 4, c % 4
        yT = np.asarray(R[c]["yT"])
        y_prompt[b, j * 2048:(j + 1) * 2048, :] = yT[:, :2048].T
        y_sample[c] = yT[:, 2048:].T
        osd = np.asarray(R[c]["o_sd"])
        d_st[:, :, 2 * c:2 * c + 2] = osd.transpose(0, 1, 3, 2, 4)
        ocd = np.asarray(R[c]["o_cd"]).reshape(DEPTH, 10, 128, 3, 2, 3)
        c_d[:, :, :, :, 2 * c:2 * c + 2, :] = ocd.transpose(0, 1, 5, 3, 4, 2)
        ol = np.asarray(R[c]["o_l"])
        l_st[:, :, 2 * c:2 * c + 2, :] = ol.transpose(0, 1, 3, 2)
        ocl = np.asarray(R[c]["o_cl"])
        c_l[:, :, :, 2 * c:2 * c + 2, :] = ocl.transpose(0, 1, 4, 3, 2)
    c_d = c_d.reshape(DEPTH, 10, 3, 6144)
    l_st = l_st.reshape(DEPTH, 10, 2048)
    c_l = c_l.reshape(DEPTH, 10, 3, 2048)
    return (y_prompt, y_sample,
            np.ascontiguousarray(d_st[:, 0:2]), np.ascontiguousarray(c_d[:, 0:2]), np.ascontiguousarray(l_st[:, 0:2]), np.ascontiguousarray(c_l[:, 0:2]),
            np.ascontiguousarray(d_st[:, 2:10]), np.ascontiguousarray(c_d[:, 2:10]), np.ascontiguousarray(l_st[:, 2:10]), np.ascontiguousarray(c_l[:, 2:10]))
```
